# Optimizing a Trainium2 kernel written in Bass

```python
import math
import jax, jax.numpy as jnp
from jax import lax
import numpy as np

D_MODEL = 1024
BATCH = 2
SEQ = 16384
DEPTH = 2

MIX_WIDTH = D_MODEL
GM_GROUPS = 4
GM_CH = MIX_WIDTH // 2 // GM_GROUPS
GM_CHUNK = 128
GM_WIDTH = GM_GROUPS * GM_CH
ATT_HEADS = 8
ATT_HD = MIX_WIDTH // 2 // ATT_HEADS
ATT_WIDTH = ATT_HEADS * ATT_HD
DILATED_PATTERNS = ((128, 1), (512, 4), (2048, 16))
ATT_BLOCK = 128
DIL_PAD = ATT_BLOCK * 16
IN_EVEN = 2 * GM_WIDTH + 3 * ATT_WIDTH + MIX_WIDTH
RW_HEAD = 64
RW_HEADS = D_MODEL // RW_HEAD
RW_DECAY_LORA = 64
RW_AAA_LORA = 64
RMS_EPS = 1e-6
LNX_EPS = 64e-5
N_EVEN = (DEPTH + 1) // 2
N_ODD = DEPTH // 2

kernel_name = "hybrid_gmlp_dilated_alibi_rwkv7"


def rmsnorm(x, g):
    xf = x.astype(jnp.float32)
    y = xf * lax.rsqrt(jnp.mean(xf * xf, axis=-1, keepdims=True) + RMS_EPS)
    return (y * g.astype(jnp.float32)).astype(x.dtype)


def alibi_slopes(n_heads):
    s = 2.0 ** (-8.0 * np.arange(1, n_heads + 1) / n_heads)
    return jnp.asarray(s, dtype=jnp.float32)


def dilated_window_attention(q, k, v, window, dilation):
    B, S, H, hd = q.shape
    L = S // dilation
    nblk = L // ATT_BLOCK
    reach = window // dilation

    def blocks(t):
        return t.reshape(B, nblk, ATT_BLOCK, dilation, H, hd)

    def with_prev(t):
        prev = jnp.concatenate([jnp.zeros_like(t[:, :1]), t[:, :-1]], axis=1)
        return jnp.concatenate([prev, t], axis=2)

    qb = blocks(q)
    kc = with_prev(blocks(k))
    vc = with_prev(blocks(v))
    s = jnp.einsum('bnqrhc,bnkrhc->bnrhqk', qb, kc).astype(jnp.float32) / math.sqrt(hd)

    qi = jnp.arange(ATT_BLOCK)[:, None]
    kloc = jnp.arange(2 * ATT_BLOCK)[None, :] - ATT_BLOCK
    j = qi - kloc
    blk = jnp.arange(nblk)[:, None, None]
    valid = (j >= 0) & (j <= reach) & (blk * ATT_BLOCK + kloc[None] >= 0)
    bias = -alibi_slopes(H)[:, None, None] * (j * dilation).astype(jnp.float32)
    s = jnp.where(valid[None, :, None, None], s + bias[None, None, None], -jnp.inf)
    m = jnp.max(s, axis=-1, keepdims=True)
    p = jnp.exp(s - m)
    l = jnp.sum(p, axis=-1)
    o = jnp.einsum('bnrhqk,bnkrhc->bnqrhc', p, vc.astype(jnp.float32))
    l_t = l.transpose(0, 1, 4, 2, 3)
    o = (o / l_t[..., None]).reshape(B, S, H, hd)
    m_t = m[..., 0].transpose(0, 1, 4, 2, 3).reshape(B, S, H)
    return o, m_t, l_t.reshape(B, S, H)


def mixed_dilated_attention(q, k, v):
    B, S, H, hd = q.shape
    s_pad = -(-S // DIL_PAD) * DIL_PAD
    pad = ((0, 0), (0, s_pad - S), (0, 0), (0, 0))
    qp, kp, vp = jnp.pad(q, pad), jnp.pad(k, pad), jnp.pad(v, pad)
    outs = [dilated_window_attention(qp, kp, vp, w, d) for (w, d) in DILATED_PATTERNS]
    m_all = jnp.stack([o[1] for o in outs], axis=0)
    l_all = jnp.stack([o[2] for o in outs], axis=0)
    o_all = jnp.stack([o[0] for o in outs], axis=0)
    wts = l_all * jnp.exp(m_all - jnp.max(m_all, axis=0, keepdims=True))
    out = jnp.sum(wts[..., None] * o_all, axis=0) / jnp.sum(wts, axis=0)[..., None]
    return out[:, :S].astype(q.dtype)


def even_mixer(h, w_in, gm_norm, gm_ws, gm_b, w_out):
    B, S, _ = h.shape
    proj = h @ w_in
    splits = list(np.cumsum([GM_WIDTH, GM_WIDTH, ATT_WIDTH, ATT_WIDTH, ATT_WIDTH]))
    u, va, q, k, vb, z = jnp.split(proj, splits, axis=-1)
    u = jax.nn.gelu(u)
    va = rmsnorm(jax.nn.gelu(va).reshape(B, S, GM_GROUPS, GM_CH), gm_norm)
    vch = va.reshape(B, S // GM_CHUNK, GM_CHUNK, GM_GROUPS, GM_CH)
    ws = gm_ws * jnp.tril(jnp.ones((GM_CHUNK, GM_CHUNK), dtype=gm_ws.dtype))[None]
    spatial = jnp.einsum('gts,bnsgc->bntgc', ws, vch) + gm_b.T[:, :, None]
    a_out = u * spatial.reshape(B, S, GM_WIDTH)
    heads = lambda t: t.reshape(B, S, ATT_HEADS, ATT_HD)
    b_out = mixed_dilated_attention(heads(q), heads(k), heads(vb)).reshape(B, S, ATT_WIDTH)
    y = jnp.concatenate([a_out, b_out], axis=-1) * jax.nn.silu(z)
    return y @ w_out


def rwkv7_scan(r, w, k, v, kk, a):
    _, B, H, N = r.shape

    def step(state, inp):
        r_t, w_t, k_t, v_t, kk_t, a_t = inp
        sa = jnp.einsum('bhvk,bhk->bhv', state, -kk_t)
        state = (state * w_t[:, :, None, :]
                 + sa[..., None] * (kk_t * a_t)[:, :, None, :]
                 + v_t[..., None] * k_t[:, :, None, :])
        y_t = jnp.einsum('bhvk,bhk->bhv', state, r_t)
        return state, y_t

    state0 = jnp.zeros((B, H, N, N), dtype=jnp.float32)
    _, ys = lax.scan(step, state0, (r, w, k, v, kk, a))
    return ys


def odd_mixer(h, mu, w_r, w_k, w_v, w_g, w0, w1, w2, a0, a1, a2, k_k, k_a, r_k, lnx_w, lnx_b, w_o):
    B, S, D = h.shape
    f32 = jnp.float32
    xx = jnp.concatenate([jnp.zeros_like(h[:, :1]), h[:, :-1]], axis=1) - h
    xr, xw, xk, xv, xa, xg = [h + xx * mu[i] for i in range(6)]
    r = xr @ w_r
    k = xk @ w_k
    v = xv @ w_v
    g = xg @ w_g
    logw = -jax.nn.softplus(-(w0 + jnp.tanh(xw @ w1) @ w2).astype(f32)) - 0.5
    decay = jnp.exp(-jnp.exp(logw))
    a = jax.nn.sigmoid((a0 + (xa @ a1) @ a2).astype(f32))
    hs = lambda t: t.astype(f32).reshape(B, S, RW_HEADS, RW_HEAD)
    kk = hs(k * k_k)
    kk = kk * lax.rsqrt(jnp.maximum(jnp.sum(kk * kk, axis=-1, keepdims=True), 1e-24))
    k = k.astype(f32) * (1.0 + (a - 1.0) * k_a.astype(f32))
    rh, kh, vh, wh, ah = hs(r), hs(k), hs(v), hs(decay), hs(a)
    tm = lambda t: t.transpose(1, 0, 2, 3)
    ys = rwkv7_scan(tm(rh), tm(wh), tm(kh), tm(vh), tm(kk), tm(ah)).transpose(1, 0, 2, 3)
    mean = jnp.mean(ys, axis=-1, keepdims=True)
    var = jnp.mean((ys - mean) ** 2, axis=-1, keepdims=True)
    y = ((ys - mean) * lax.rsqrt(var + LNX_EPS)).reshape(B, S, D) * lnx_w.astype(f32) + lnx_b.astype(f32)
    bonus = jnp.sum(rh * kh * r_k.astype(f32), axis=-1, keepdims=True) * vh
    y = (y + bonus.reshape(B, S, D)).astype(h.dtype) * jax.nn.silu(g)
    return y @ w_o


def setup_inputs(seed: int = 0) -> dict:
    key = jax.random.key(seed)
    ks = iter(jax.random.split(key, 40))
    nrm = lambda shape, scale: scale * jax.random.normal(next(ks), shape, dtype=jnp.float32)
    D, Ne, No, H, N = D_MODEL, N_EVEN, N_ODD, RW_HEADS, RW_HEAD
    return {
        "x": nrm((BATCH, SEQ, D), 1.0),
        "ln_even": 1.0 + nrm((Ne, D), 0.02),
        "w_in_even": nrm((Ne, D, IN_EVEN), D ** -0.5),
        "gm_norm": 1.0 + nrm((Ne, GM_GROUPS, GM_CH), 0.02),
        "gm_ws": nrm((Ne, GM_GROUPS, GM_CHUNK, GM_CHUNK), 0.5 * GM_CHUNK ** -0.5),
        "gm_b": 1.0 + nrm((Ne, GM_GROUPS, GM_CHUNK), 0.1),
        "w_out_even": nrm((Ne, MIX_WIDTH, D), 0.5 * MIX_WIDTH ** -0.5),
        "ln_odd": 1.0 + nrm((No, D), 0.02),
        "rw_mu": jax.random.uniform(next(ks), (No, 6, D), dtype=jnp.float32),
        "rw_wr": nrm((No, D, D), D ** -0.5),
        "rw_wk": nrm((No, D, D), D ** -0.5),
        "rw_wv": nrm((No, D, D), D ** -0.5),
        "rw_wg": nrm((No, D, D), D ** -0.5),
        "rw_w0": jax.random.uniform(next(ks), (No, D), dtype=jnp.float32, minval=-5.0, maxval=1.0),
        "rw_w1": nrm((No, D, RW_DECAY_LORA), D ** -0.5),
        "rw_w2": nrm((No, RW_DECAY_LORA, D), 0.1 * RW_DECAY_LORA ** -0.5),
        "rw_a0": nrm((No, D), 0.1),
        "rw_a1": nrm((No, D, RW_AAA_LORA), D ** -0.5),
        "rw_a2": nrm((No, RW_AAA_LORA, D), 0.1 * RW_AAA_LORA ** -0.5),
        "rw_kk": 0.85 + nrm((No, D), 0.02),
        "rw_ka": 1.0 + nrm((No, D), 0.02),
        "rw_rk": nrm((No, H, N), 0.1),
        "rw_lnw": 1.0 + nrm((No, D), 0.02),
        "rw_lnb": nrm((No, D), 0.01),
        "rw_wo": nrm((No, D, D), 0.5 * D ** -0.5),
        "final_norm": 1.0 + nrm((D,), 0.02),
    }


def reference(x, ln_even, w_in_even, gm_norm, gm_ws, gm_b, w_out_even,
              ln_odd, rw_mu, rw_wr, rw_wk, rw_wv, rw_wg, rw_w0, rw_w1, rw_w2,
              rw_a0, rw_a1, rw_a2, rw_kk, rw_ka, rw_rk, rw_lnw, rw_lnb, rw_wo,
              final_norm):
    for layer in range(DEPTH):
        i = layer // 2
        if layer % 2 == 0:
            h = rmsnorm(x, ln_even[i])
            x = x + even_mixer(h, w_in_even[i], gm_norm[i], gm_ws[i], gm_b[i], w_out_even[i])
        else:
            h = rmsnorm(x, ln_odd[i])
            x = x + odd_mixer(h, rw_mu[i], rw_wr[i], rw_wk[i], rw_wv[i], rw_wg[i],
                              rw_w0[i], rw_w1[i], rw_w2[i], rw_a0[i], rw_a1[i], rw_a2[i],
                              rw_kk[i], rw_ka[i], rw_rk[i], rw_lnw[i], rw_lnb[i], rw_wo[i])
    return rmsnorm(x, final_norm)
```

```python
import math
from contextlib import ExitStack

import numpy as np
import ml_dtypes

import concourse.bass as bass
import concourse.mybir as mybir
from concourse.bass_utils import run_bass_kernel_spmd

F32 = mybir.dt.float32
BF16 = mybir.dt.bfloat16
ALU = mybir.AluOpType
AF = mybir.ActivationFunctionType
AX = mybir.AxisListType

NPBF16 = ml_dtypes.bfloat16

ENGS = ("pe", "dve", "act", "pool", "sp")
ROT = 12000

D = 1024
RMS_EPS = 1e-6
LNX_EPS = 64e-5


class Tok:
    __slots__ = ("name", "w", "r")

    def __init__(self, name=""):
        self.name = name
        self.w = {}
        self.r = {}


class Prog:
    def __init__(self, nc):
        self.nc = nc
        self.es = ExitStack()
        self.stream = {e: [] for e in ENGS}
        self.count = {e: 0 for e in ENGS}
        self.seen = {e: {} for e in ENGS}
        self.dmasem = {}
        self.semkeys = []
        self._semset = set()
        self.n_tok = 0
        self.cut = False
        self.stop_at = None

    def mark(self, name):
        if self.stop_at is not None and name == self.stop_at:
            self.cut = True

    def sb(self, name, shape, dt):
        return self.es.enter_context(self.nc.sbuf_tensor(name, list(shape), dt))

    def ps(self, name, shape, dt):
        return self.es.enter_context(self.nc.psum_tensor(name, list(shape), dt))

    def tok(self, name=""):
        self.n_tok += 1
        return Tok(name or f"t{self.n_tok}")

    def _key(self, k):
        if k not in self._semset:
            self._semset.add(k)
            self.semkeys.append(k)
        return k

    def _deps(self, reads, writes):
        deps = {}
        for t in reads:
            for k, v in t.w.items():
                if deps.get(k, -1) < v:
                    deps[k] = v
        for t in writes:
            for src in (t.w, t.r):
                for k, v in src.items():
                    if deps.get(k, -1) < v:
                        deps[k] = v
        return deps

    def _emit_waits(self, eng, deps, is_dma):
        st = self.stream[eng]
        seen = self.seen[eng]
        for k, v in deps.items():
            if k[0] == "E":
                src = k[1]
                if src == eng and not is_dma:
                    if eng == "pe":
                        continue
                    if eng != "pool" and self.count[eng] - 1 - v > 2:
                        continue
                if seen.get(k, -1) >= v:
                    continue
                seen[k] = v
                key = self._key(("E", src, v // ROT))
                st.append(("wait", key, v % ROT + 1))
            else:
                if seen.get(k, -1) >= v:
                    continue
                seen[k] = v
                st.append(("wait", k, v))

    def _update(self, ev, reads, writes):
        k, v = ev
        wset = set(id(t) for t in writes)
        for t in reads:
            if id(t) not in wset:
                if t.r.get(k, -1) < v:
                    t.r[k] = v
        rset = set(id(t) for t in reads)
        for t in writes:
            if t.r or id(t) in rset:
                t.w = {k: v}
                t.r = {}
            else:
                if t.w.get(k, -1) < v:
                    t.w[k] = v

    def op(self, eng, name, reads=(), writes=(), **kw):
        if self.cut:
            return
        deps = self._deps(reads, writes)
        self._emit_waits(eng, deps, False)
        idx = self.count[eng]
        self.count[eng] += 1
        key = self._key(("E", eng, idx // ROT))
        self.stream[eng].append(("op", name, kw, key, 1))
        self._update((("E", eng), idx), reads, writes)

    def dma(self, eng, slot, reads=(), writes=(), **kw):
        if self.cut:
            return
        deps = self._deps(reads, writes)
        self._emit_waits(eng, deps, True)
        if id(slot) not in self.dmasem:
            self.dmasem[id(slot)] = [self._key(("D", len(self.dmasem))), 0]
        ent = self.dmasem[id(slot)]
        ent[1] += 16
        self.stream[eng].append(("op", "dma_start", kw, ent[0], 16))
        self._update((ent[0], ent[1]), reads, writes)

    def finish(self):
        st = self.stream["sp"]
        for key, val in self.dmasem.values():
            st.append(("wait", key, val))

    def emit(self):
        nc = self.nc
        sems = {}
        for k in self.semkeys:
            nm = "s_" + "_".join(str(x) for x in k)
            sems[k] = self.es.enter_context(nc.semaphore(nm))
        streams = self.stream

        def run(engine, name):
            for it in streams[name]:
                if it[0] == "wait":
                    engine.wait_ge(sems[it[1]], it[2])
                else:
                    ins = getattr(engine, it[1])(**it[2])
                    ins.then_inc(sems[it[3]], it[4])

        with nc.Block() as block:
            @block.tensor
            def _(e):
                run(e, "pe")

            @block.vector
            def _(e):
                run(e, "dve")

            @block.scalar
            def _(e):
                run(e, "act")

            @block.gpsimd
            def _(e):
                run(e, "pool")

            @block.sync
            def _(e):
                run(e, "sp")
        self.es.close()


def build_phase_b(TS, mode):
    nc = bass.Bass("TRN2", target_bir_lowering=False)
    yT = nc.dram_tensor("yT", [8, 128, TS], BF16, kind="ExternalInput").ap()
    xres = nc.dram_tensor("xres", [TS, D], F32, kind="ExternalInput").ap()
    w = nc.dram_tensor("w", [D, D], F32, kind="ExternalInput").ap()
    g = nc.dram_tensor("g", [D], F32, kind="ExternalInput").ap()
    ident = nc.dram_tensor("ident", [128, 128], BF16, kind="ExternalInput").ap()
    if mode == "mid":
        xo = nc.dram_tensor("xo", [TS, D], F32, kind="ExternalOutput").ap()
        hT = nc.dram_tensor("hT", [8, 128, TS], BF16, kind="ExternalOutput").ap()
    else:
        out = nc.dram_tensor("out", [TS, D], F32, kind="ExternalOutput").ap()
    P = Prog(nc)
    phase_b_body(P, TS, mode, yT, xres, w, g, ident,
                 xo if mode == "mid" else None, hT if mode == "mid" else None,
                 out if mode == "fin" else None)
    P.finish()
    P.emit()
    return nc


def load_weight_bf16(P, w_ap, K, N, name, dma_eng="sp", pad_to=None):
    KC = K // 128
    wb = P.sb(name, [128, KC, pad_to or N], BF16)
    wb_t = P.tok(name)
    if pad_to:
        P.op("pool", "memset", writes=[wb_t], ap=wb[:], constant=0.0)
    stg = [P.sb(f"{name}_stg{i}", [128, N], F32) for i in range(2)]
    stg_t = [P.tok() for _ in range(2)]
    for kc in range(KC):
        s = kc % 2
        P.dma(dma_eng, stg_t[s], writes=[stg_t[s]],
              out=stg[s][:], in_=w_ap[kc * 128:(kc + 1) * 128, :])
        eng = "pool" if kc % 2 == 0 else "dve"
        P.op(eng, "tensor_copy", reads=[stg_t[s]], writes=[wb_t], out=wb[:, kc, 0:N], in_=stg[s][:])
    return wb, wb_t


def phase_b_body(P, TS, mode, yT, xres, w, g, ident, xo, hT, out):
    NT = TS // 128
    wb, wb_t = load_weight_bf16(P, w, D, D, "wb")
    idb = P.sb("idb", [128, 128], BF16)
    idb_t = P.tok("idb")
    P.dma("sp", idb_t, writes=[idb_t], out=idb[:], in_=ident)
    if mode == "mid":
        gcol = P.sb("gcol", [128, 8], F32)
        gcol_t = P.tok("gcol")
        P.dma("sp", gcol_t, writes=[gcol_t], out=gcol[:], in_=g.rearrange("(kc p) -> p kc", p=128),
              allow_slow_non_contiguous=True)
    else:
        gbc = P.sb("gbc", [128, D], F32)
        gbc_t = P.tok("gbc")
        P.dma("sp", gbc_t, writes=[gbc_t], out=gbc[:], in_=g.partition_broadcast(128))
    epst = P.sb("epst", [128, 1], F32)
    eps_t = P.tok("eps")
    P.op("dve", "memset", writes=[eps_t], ap=epst[:], constant=RMS_EPS)

    NB = 2
    yt = [P.sb(f"yt{i}", [128, 8, 128], BF16) for i in range(NB)]
    yt_t = [P.tok() for _ in range(NB)]
    xr = [P.sb(f"xr{i}", [128, D], F32) for i in range(NB)]
    xr_t = [P.tok() for _ in range(NB)]
    x1 = [P.sb(f"x1{i}", [128, D], F32) for i in range(NB)]
    x1_t = [P.tok() for _ in range(NB)]
    junk = P.sb("junk", [128, D], F32)
    junk_t = P.tok("junk")
    ss = [P.sb(f"ss{i}", [128, 1], F32) for i in range(NB)]
    ss_t = [P.tok() for _ in range(NB)]
    rs = [P.sb(f"rs{i}", [128, 1], F32) for i in range(NB)]
    rs_t = [P.tok() for _ in range(NB)]
    ps = [P.ps(f"ps{i}", [128, D], F32) for i in range(NB)]
    ps_t = [P.tok() for _ in range(NB)]
    if mode == "mid":
        xn = [P.sb(f"xn{i}", [128, D], BF16) for i in range(NB)]
        xn_t = [P.tok() for _ in range(NB)]
        pst = [P.ps(f"pst{i}", [128, 8, 128], BF16) for i in range(NB)]
        pst_t = [P.tok() for _ in range(NB)]
        ho = [P.sb(f"ho{i}", [128, 8, 128], BF16) for i in range(NB)]
        ho_t = [P.tok() for _ in range(NB)]
    else:
        oo = [P.sb(f"oo{i}", [128, D], F32) for i in range(NB)]
        oo_t = [P.tok() for _ in range(NB)]

    def load(i):
        b = i % NB
        t0 = i * 128
        P.dma("sp", yt_t[b], writes=[yt_t[b]],
              out=yt[b][:], in_=yT[:, :, t0:t0 + 128].rearrange("kc p t -> p kc t"))
        P.dma("sp", xr_t[b], writes=[xr_t[b]], out=xr[b][:], in_=xres[t0:t0 + 128, :])

    load(0)
    for i in range(NT):
        b = i % NB
        t0 = i * 128
        if i + 1 < NT:
            load(i + 1)
        for half in range(2):
            for kc in range(8):
                P.op("pe", "matmul", reads=[yt_t[b], wb_t], writes=[ps_t[b]],
                     out=ps[b][:, half * 512:(half + 1) * 512], lhsT=yt[b][:, kc, :],
                     rhs=wb[:, kc, half * 512:(half + 1) * 512], start=(kc == 0), stop=(kc == 7))
        P.op("dve", "tensor_tensor", reads=[ps_t[b], xr_t[b]], writes=[x1_t[b]],
             out=x1[b][:], in0=ps[b][:], in1=xr[b][:], op=ALU.add)
        P.op("act", "activation", reads=[x1_t[b]], writes=[junk_t, ss_t[b]],
             out=junk[:], in_=x1[b][:], func=AF.Square, accum_out=ss[b][:])
        P.op("act", "activation", reads=[ss_t[b], eps_t], writes=[rs_t[b]],
             out=rs[b][:], in_=ss[b][:], func=AF.Sqrt, bias=epst[:], scale=1.0 / D)
        P.op("dve", "reciprocal", reads=[rs_t[b]], writes=[rs_t[b]], out=rs[b][:], in_=rs[b][:])
        if mode == "mid":
            P.dma("pool", x1_t[b], reads=[x1_t[b]], out=xo[t0:t0 + 128, :], in_=x1[b][:])
            P.op("dve", "tensor_scalar", reads=[x1_t[b], rs_t[b]], writes=[xn_t[b]],
                 out=xn[b][:], in0=x1[b][:], scalar1=rs[b][:], scalar2=None, op0=ALU.mult)
            for kc in range(8):
                P.op("pe", "transpose", reads=[xn_t[b], idb_t], writes=[pst_t[b]],
                     out=pst[b][:, kc, :], in_=xn[b][:, kc * 128:(kc + 1) * 128], identity=idb[:])
            P.op("dve", "tensor_tensor", reads=[pst_t[b], gcol_t], writes=[ho_t[b]],
                 out=ho[b][:], in0=pst[b][:], in1=gcol[:].unsqueeze(2).broadcast_to([128, 8, 128]),
                 op=ALU.mult)
            P.dma("pool", ho_t[b], reads=[ho_t[b]],
                  out=hT[:, :, t0:t0 + 128].rearrange("kc p t -> p kc t"), in_=ho[b][:])
        else:
            P.op("dve", "scalar_tensor_tensor", reads=[x1_t[b], rs_t[b], gbc_t], writes=[oo_t[b]],
                 out=oo[b][:], in0=x1[b][:], scalar=rs[b][:], in1=gbc[:], op0=ALU.mult, op1=ALU.mult)
            P.dma("pool", oo_t[b], reads=[oo_t[b]], out=out[t0:t0 + 128, :], in_=oo[b][:])


GELU_C = 1.5957691216057308
PATTERNS = (1, 4, 16)


def build_phase_a(T, stop_at=None):
    nc = bass.Bass("TRN2", target_bir_lowering=False)
    x = nc.dram_tensor("x", [T, D], F32, kind="ExternalInput").ap()
    ln = nc.dram_tensor("ln", [D], F32, kind="ExternalInput").ap()
    w_in_s = nc.dram_tensor("w_in_s", [D, 896], F32, kind="ExternalInput").ap()
    gmn = nc.dram_tensor("gmn", [128], F32, kind="ExternalInput").ap()
    gwsT = nc.dram_tensor("gwsT", [128, 128], F32, kind="ExternalInput").ap()
    gmb = nc.dram_tensor("gmb", [128], F32, kind="ExternalInput").ap()
    triu = nc.dram_tensor("triu", [128, 128], F32, kind="ExternalInput").ap()
    etab = nc.dram_tensor("etab", [128, 3, 512], F32, kind="ExternalInput").ap()
    ident = nc.dram_tensor("ident", [128, 128], BF16, kind="ExternalInput").ap()
    yT = nc.dram_tensor("yT", [2, 128, T], BF16, kind="ExternalOutput").ap()
    P = Prog(nc)
    P.stop_at = stop_at
    phase_a_body(P, T, x, ln, w_in_s, gmn, gwsT, gmb, triu, etab, ident, yT)
    P.finish()
    P.emit()
    return nc


def phase_a_body(P, T, x, ln, w_in_s, gmn, gwsT, gmb, triu, etab, ident, yT):
    NW = T // 2048
    wb, wb_t = load_weight_bf16(P, w_in_s, D, 896, "wbin")

    def const(name, shape, dt, src, **kw):
        t = P.sb(name, shape, dt)
        tk = P.tok(name)
        P.dma("sp", tk, writes=[tk], out=t[:], in_=src, **kw)
        return t, tk

    idb, idb_t = const("idb", [128, 128], BF16, ident)
    lncol, lncol_t = const("lncol", [128, 8], F32, ln.rearrange("(kc p) -> p kc", p=128),
                           allow_slow_non_contiguous=True)
    gmn_bc, gmn_t = const("gmn_bc", [128, 128], F32, gmn.partition_broadcast(128))
    gb_bc, gb_t = const("gb_bc", [128, 128], F32, gmb.partition_broadcast(128))
    ws_f, ws_t = const("ws_f", [128, 128], F32, gwsT)
    tri_f, tri_t = const("tri_f", [128, 128], F32, triu)
    E, E_t = const("E", [128, 3, 512], F32, etab)
    wsm = P.sb("wsm", [128, 128], BF16)
    wsm_t = P.tok("wsm")
    P.op("dve", "tensor_tensor", reads=[ws_t, tri_t], writes=[wsm_t], out=wsm[:], in0=ws_f[:], in1=tri_f[:],
         op=ALU.mult)
    epst = P.sb("epst", [128, 1], F32)
    eps_t = P.tok("eps")
    P.op("dve", "memset", writes=[eps_t], ap=epst[:], constant=RMS_EPS)
    ones = P.sb("ones", [128, 128], BF16)
    ones_t = P.tok("ones")
    P.op("dve", "memset", writes=[ones_t], ap=ones[:], constant=1.0)

    P.mark("consts")
    xt = [P.sb(f"xt{i}", [128, D], F32) for i in range(2)]
    xt_t = [P.tok() for _ in range(2)]
    junk = P.sb("junk", [128, D], BF16)
    junk_t = P.tok("junk")
    ss = [P.sb(f"ss{i}", [128, 1], F32) for i in range(2)]
    ss_t = [P.tok() for _ in range(2)]
    xn = [P.sb(f"xn{i}", [128, D], BF16) for i in range(2)]
    xn_t = [P.tok() for _ in range(2)]
    hT = [P.sb(f"hT{i}", [128, 8, 512], BF16) for i in range(2)]
    hT_t = [P.tok() for _ in range(2)]
    qT = P.sb("qT", [128, 2, 2, 2048], BF16)
    kT = P.sb("kT", [128, 2, 2048], BF16)
    vT = P.sb("vT", [128, 2, 2048], BF16)
    qT_t = [P.tok() for _ in range(2)]
    for s_ in range(2):
        P.op("pool", "memset", writes=[qT_t[s_]], ap=qT[:, s_, :, :], constant=0.0)
    kT_t = [P.tok() for _ in range(2)]
    vT_t = [P.tok() for _ in range(2)]
    szb = P.sb("szb", [128, 2, 2048], F32)
    szb_t = [P.tok() for _ in range(2)]
    gu = [P.sb(f"gu{i}", [128, 512], F32) for i in range(2)]
    gu_t = [P.tok() for _ in range(2)]
    sza = [P.sb(f"sza{i}", [128, 512], F32) for i in range(2)]
    sza_t = [P.tok() for _ in range(2)]
    gtmp = P.sb("gtmp", [128, 512], F32)
    gtmp_t = P.tok()
    ya = [P.sb(f"ya{i}", [128, 512], BF16) for i in range(2)]
    ya_t = [P.tok() for _ in range(2)]
    yb = [P.sb(f"yb{i}", [128, 2048], BF16) for i in range(2)]
    yb_t = [P.tok() for _ in range(2)]
    vtok = P.sb("vtok", [128, 2, 48, 128], BF16)
    vtok_t = [P.tok() for _ in range(2)]
    acc = P.sb("acc", [128, 2, 2048], F32)
    acc_t = P.tok("acc")
    expS = [P.sb(f"expS{i}", [128, 512], F32) for i in range(2)]
    expS_t = [P.tok() for _ in range(2)]
    Pm = [P.sb(f"Pm{i}", [128, 512], BF16) for i in range(2)]
    Pm_t = [P.tok() for _ in range(2)]
    va_a = P.sb("va_a", [128, 128], F32)
    va_a_t = P.tok()
    va_b = P.sb("va_b", [128, 128], F32)
    va_b_t = P.tok()
    van = P.sb("van", [128, 128], BF16)
    van_t = P.tok()
    ss2 = P.sb("ss2", [128, 1], F32)
    ss2_t = P.tok()
    sp_a = P.sb("sp_a", [128, 128], F32)
    sp_a_t = P.tok()

    pst = P.ps("pst", [128, 8, 128], BF16)
    pst_t = P.tok()
    pj = [P.ps(f"pj{i}", [128, 512], F32) for i in range(2)]
    pj_t = [P.tok() for _ in range(2)]
    pva = P.ps("pva", [128, 4, 128], F32)
    pva_t = P.tok()
    psp_t = pva_t
    pvt = P.ps("pvt", [128, 8, 128], BF16)
    pvt_t = P.tok()
    pS = [P.ps(f"pS{i}", [128, 512], F32) for i in range(2)]
    pS_t = [P.tok() for _ in range(2)]
    pOL = P.ps("pOL", [128, 4, 128], F32)
    pOL_t = P.tok()

    def load_x(i):
        b = i % 2
        P.dma("sp", xt_t[b], writes=[xt_t[b]], out=xt[b][:], in_=x[i * 128:(i + 1) * 128, :])

    def gelu_from_psum(ps_ap, ps_tok, shape, tmp, tmp_t, out_ap, out_tok):
        P.op("act", "activation", reads=[ps_tok], writes=[tmp_t], out=tmp, in_=ps_ap, func=AF.Square)
        P.op("dve", "tensor_scalar", reads=[tmp_t], writes=[tmp_t], out=tmp, in0=tmp, scalar1=0.044715,
             scalar2=1.0, op0=ALU.mult, op1=ALU.add)
        P.op("dve", "tensor_tensor", reads=[tmp_t, ps_tok], writes=[tmp_t], out=tmp, in0=tmp, in1=ps_ap,
             op=ALU.mult)
        P.op("act", "activation", reads=[tmp_t], writes=[tmp_t], out=tmp, in_=tmp, func=AF.Sigmoid,
             scale=GELU_C)
        P.op("dve", "tensor_tensor", reads=[tmp_t, ps_tok], writes=[out_tok], out=out_ap, in0=tmp, in1=ps_ap,
             op=ALU.mult)

    pj_ctr = [0]
    pS_ctr = [0]
    load_x(0)
    ntile_total = T // 128
    for w in range(NW):
        s = w % 2
        for grp in range(4):
            gi = w * 4 + grp
            gb = gi % 2
            c0 = grp * 512
            for tt in range(4):
                i = gi * 4 + tt
                b = i % 2
                if i + 1 < ntile_total:
                    load_x(i + 1)
                P.op("act", "activation", reads=[xt_t[b]], writes=[junk_t, ss_t[b]],
                     out=junk[:], in_=xt[b][:], func=AF.Square, accum_out=ss[b][:])
                P.op("act", "activation", reads=[ss_t[b], eps_t], writes=[ss_t[b]],
                     out=ss[b][:], in_=ss[b][:], func=AF.Sqrt, bias=epst[:], scale=1.0 / D)
                P.op("dve", "reciprocal", reads=[ss_t[b]], writes=[ss_t[b]], out=ss[b][:], in_=ss[b][:])
                P.op("dve", "tensor_scalar", reads=[xt_t[b], ss_t[b]], writes=[xn_t[b]],
                     out=xn[b][:], in0=xt[b][:], scalar1=ss[b][:], scalar2=None, op0=ALU.mult)
                for kc in range(8):
                    P.op("pe", "transpose", reads=[xn_t[b], idb_t], writes=[pst_t],
                         out=pst[:, kc, :], in_=xn[b][:, kc * 128:(kc + 1) * 128], identity=idb[:])
                P.op("dve", "tensor_tensor", reads=[pst_t, lncol_t], writes=[hT_t[gb]],
                     out=hT[gb][:, :, tt * 128:(tt + 1) * 128], in0=pst[:],
                     in1=lncol[:].unsqueeze(2).broadcast_to([128, 8, 128]), op=ALU.mult)
            P.mark("tiles")
            for name, col in (("u", 0), ("za", 640), ("q", 256), ("k", 384), ("v", 512), ("zb", 768)):
                pb = pj_ctr[0] % 2
                pj_ctr[0] += 1
                for kc in range(8):
                    P.op("pe", "matmul", reads=[wb_t, hT_t[gb]], writes=[pj_t[pb]],
                         out=pj[pb][:], lhsT=wb[:, kc, col:col + 128], rhs=hT[gb][:, kc, :],
                         start=(kc == 0), stop=(kc == 7))
                if name == "u":
                    gelu_from_psum(pj[pb][:], pj_t[pb], None, gtmp[:], gtmp_t, gu[gb][:], gu_t[gb])
                elif name == "za":
                    P.op("act", "activation", reads=[pj_t[pb]], writes=[sza_t[gb]], out=sza[gb][:],
                         in_=pj[pb][:], func=AF.Silu)
                    P.op("pool", "tensor_tensor", reads=[gu_t[gb], sza_t[gb]], writes=[gu_t[gb]],
                         out=gu[gb][:], in0=gu[gb][:], in1=sza[gb][:], op=ALU.mult)
                elif name == "zb":
                    P.op("act", "activation", reads=[pj_t[pb]], writes=[szb_t[s]],
                         out=szb[:, s, c0:c0 + 512], in_=pj[pb][:], func=AF.Silu)
                elif name == "q":
                    for hh in range(2):
                        hp_ = slice(hh * 64, hh * 64 + 64)
                        P.op("dve", "tensor_copy", reads=[pj_t[pb]], writes=[qT_t[s]],
                             out=qT[hp_, s, hh, c0:c0 + 512], in_=pj[pb][hp_, :])
                else:
                    dst, dtk = {"k": (kT, kT_t), "v": (vT, vT_t)}[name]
                    P.op("dve", "tensor_copy", reads=[pj_t[pb]], writes=[dtk[s]],
                         out=dst[:, s, c0:c0 + 512], in_=pj[pb][:])
            P.mark("proj")
            for tt in range(4):
                tcol = slice(tt * 128, (tt + 1) * 128)
                for kc in range(8):
                    P.op("pe", "matmul", reads=[wb_t, hT_t[gb]], writes=[pva_t],
                         out=pva[:, 0, :], lhsT=hT[gb][:, kc, tcol], rhs=wb[:, kc, 128:256],
                         start=(kc == 0), stop=(kc == 7))
                gelu_from_psum(pva[:, 0, :], pva_t, None, va_a[:], va_a_t, va_b[:], va_b_t)
                P.op("act", "activation", reads=[va_b_t], writes=[va_a_t, ss2_t],
                     out=va_a[:], in_=va_b[:], func=AF.Square, accum_out=ss2[:])
                P.op("act", "activation", reads=[ss2_t, eps_t], writes=[ss2_t],
                     out=ss2[:], in_=ss2[:], func=AF.Sqrt, bias=epst[:], scale=1.0 / 128)
                P.op("dve", "reciprocal", reads=[ss2_t], writes=[ss2_t], out=ss2[:], in_=ss2[:])
                P.op("dve", "scalar_tensor_tensor", reads=[va_b_t, ss2_t, gmn_t], writes=[van_t],
                     out=van[:], in0=va_b[:], scalar=ss2[:], in1=gmn_bc[:], op0=ALU.mult, op1=ALU.mult)
                P.op("pe", "matmul", reads=[van_t, wsm_t], writes=[psp_t],
                     out=pva[:, 1, :], lhsT=van[:], rhs=wsm[:], start=True, stop=True)
                P.op("dve", "tensor_tensor", reads=[psp_t, gb_t], writes=[sp_a_t],
                     out=sp_a[:], in0=pva[:, 1, :], in1=gb_bc[:], op=ALU.add)
                P.op("dve", "tensor_tensor", reads=[sp_a_t, gu_t[gb]], writes=[ya_t[gb]],
                     out=ya[gb][:, tcol], in0=sp_a[:], in1=gu[gb][:, tcol], op=ALU.mult)
            t0 = gi * 512
            P.dma("pool", ya_t[gb], reads=[ya_t[gb]], out=yT[0, :, t0:t0 + 512], in_=ya[gb][:])

        P.mark("gmlp")
        blocks = []
        for p, d in enumerate(PATTERNS):
            nl_per = 16 // d if d < 16 else 1
            for r in range(d):
                for nl in range(2048 // (128 * d)):
                    blocks.append((p, d, r, nl))
        def vt_index(p, d, r, nl):
            return {1: nl, 4: 16 + r * 4 + nl, 16: 32 + r}[d]
        for q0 in range(0, 48, 4):
            for u4 in range(4):
                p, d, r, nl = blocks[q0 + u4]
                src = vT[:, s, :].rearrange("p (i d) -> p d i", d=d)[:, r, nl * 128:(nl + 1) * 128]
                P.op("pe", "transpose", reads=[vT_t[s], idb_t], writes=[pvt_t],
                     out=pvt[:, u4, :], in_=src, identity=idb[:])
            j0 = vt_index(*blocks[q0])
            P.op("dve", "tensor_copy", reads=[pvt_t], writes=[vtok_t[s]],
                 out=vtok[:, s, j0:j0 + 4, :], in_=pvt[:, 0:4, :])
        P.mark("vtok")
        for p, d in enumerate(PATTERNS):
            nper = 2048 // (128 * d)
            for r in range(d):
                for nl in range(nper):
                    j = vt_index(p, d, r, nl)
                    if nl > 0:
                        ps_, pnl = s, nl - 1
                    elif w > 0:
                        ps_, pnl = 1 - s, nper - 1
                    else:
                        ps_, pnl = None, None
                    has_prev = ps_ is not None
                    sb_ = pS_ctr[0] % 2
                    pS_ctr[0] += 1
                    qv = [qT[:, s, hh, :].rearrange("p (i d) -> p d i", d=d)[:, r, nl * 128:(nl + 1) * 128]
                          for hh in range(2)]
                    kcur = kT[:, s, :].rearrange("p (i d) -> p d i", d=d)[:, r, nl * 128:(nl + 1) * 128]
                    if has_prev:
                        kprev = kT[:, ps_, :].rearrange("p (i d) -> p d i", d=d)[:, r, pnl * 128:(pnl + 1) * 128]
                        jp = vt_index(p, d, r, pnl)
                    rd = [qT_t[s], kT_t[s]] + ([kT_t[ps_]] if has_prev else [])
                    for h in range(2):
                        hp = slice(h * 64, (h + 1) * 64)
                        if has_prev:
                            P.op("pe", "matmul", reads=rd, writes=[pS_t[sb_]],
                                 out=pS[sb_][:, h * 256:h * 256 + 128], lhsT=kprev, rhs=qv[h],
                                 start=True, stop=True)
                        P.op("pe", "matmul", reads=rd, writes=[pS_t[sb_]],
                             out=pS[sb_][:, h * 256 + 128:h * 256 + 256], lhsT=kcur, rhs=qv[h],
                             start=True, stop=True)
                    P.mark("att_mm")
                    if has_prev:
                        P.op("act", "activation", reads=[pS_t[sb_]], writes=[expS_t[sb_]],
                             out=expS[sb_][:], in_=pS[sb_][:], func=AF.Exp, scale=0.125)
                        P.op("dve", "tensor_tensor", reads=[expS_t[sb_], E_t], writes=[Pm_t[sb_]],
                             out=Pm[sb_][:], in0=expS[sb_][:], in1=E[:, p, :], op=ALU.mult)
                    else:
                        v4 = lambda ap: ap.rearrange("p (h c t) -> p h c t", h=2, c=2)[:, :, 1, :]
                        P.op("act", "activation", reads=[pS_t[sb_]], writes=[expS_t[sb_]],
                             out=v4(expS[sb_][:]), in_=v4(pS[sb_][:]), func=AF.Exp, scale=0.125)
                        P.op("dve", "tensor_tensor", reads=[expS_t[sb_], E_t], writes=[Pm_t[sb_]],
                             out=v4(Pm[sb_][:]), in0=v4(expS[sb_][:]), in1=v4(E[:, p, :]), op=ALU.mult)
                    P.mark("att_pm")
                    rdv = [Pm_t[sb_], vtok_t[s], ones_t] + ([vtok_t[ps_]] if has_prev else [])
                    for h in range(2):
                        for which in range(2):
                            o_ap = pOL[:, which * 2 + h, :]
                            if has_prev:
                                lp = vtok[:, ps_, jp, :] if which == 0 else ones[:]
                                P.op("pe", "matmul", reads=rdv, writes=[pOL_t], out=o_ap, lhsT=lp,
                                     rhs=Pm[sb_][:, h * 256:h * 256 + 128], start=True, stop=False)
                            lc = vtok[:, s, j, :] if which == 0 else ones[:]
                            P.op("pe", "matmul", reads=rdv, writes=[pOL_t], out=o_ap, lhsT=lc,
                                 rhs=Pm[sb_][:, h * 256 + 128:h * 256 + 256], start=(not has_prev), stop=True)
                    P.mark("att_pv")
                    for h in range(2):
                        hp = slice(h * 64, (h + 1) * 64)
                        src = pOL[hp, :, :].rearrange("p (k h) t -> p k h t", h=2)[:, :, h, :]
                        dstv = acc[hp, :, :].rearrange("p k (i d) -> p k d i", d=d)[:, :, r, nl * 128:(nl + 1) * 128]
                        if p == 0:
                            P.op("dve", "tensor_copy", reads=[pOL_t], writes=[acc_t], out=dstv, in_=src)
                        else:
                            P.op("dve", "tensor_tensor", reads=[pOL_t, acc_t], writes=[acc_t],
                                 out=dstv, in0=dstv, in1=src, op=ALU.add)
        P.mark("attn")
        P.op("dve", "reciprocal", reads=[acc_t], writes=[acc_t], out=acc[:, 1, :], in_=acc[:, 1, :])
        P.op("dve", "tensor_tensor", reads=[acc_t], writes=[acc_t], out=acc[:, 0, :], in0=acc[:, 0, :],
             in1=acc[:, 1, :], op=ALU.mult)
        P.op("dve", "tensor_tensor", reads=[acc_t, szb_t[s]], writes=[yb_t[s]], out=yb[s][:],
             in0=acc[:, 0, :], in1=szb[:, s, :], op=ALU.mult)
        P.dma("pool", yb_t[s], reads=[yb_t[s]], out=yT[1, :, w * 2048:(w + 1) * 2048], in_=yb[s][:])


DECAY_C = 0.6065306597126334
V_W0, V_A0, V_KK, V_KA, V_RK, V_LNW, V_LNB = range(7)


def build_phase_c(T, stop_at=None):
    nc = bass.Bass("TRN2", target_bir_lowering=False)
    dt = lambda n, s, d=F32: nc.dram_tensor(n, s, d, kind="ExternalInput").ap()
    a = dict(
        hT=dt("hT", [8, 128, T], BF16), mu=dt("mu", [6, D]),
        wr=dt("wr", [D, 256]), wk=dt("wk", [D, 256]), wv=dt("wv", [D, 256]), wg=dt("wg", [D, 256]),
        w1=dt("w1", [D, 64]), a1=dt("a1", [D, 64]), w2=dt("w2", [64, 256]), a2=dt("a2", [64, 256]),
        vecs=dt("vecs", [7, 256]),
        cm=dt("cm", [128, 9, 128]),
        ident=dt("ident", [128, 128], BF16),
    )
    yT = nc.dram_tensor("yT", [2, 128, T], BF16, kind="ExternalOutput").ap()
    P = Prog(nc)
    P.stop_at = stop_at
    phase_c_body(P, T, a, yT)
    P.finish()
    P.emit()
    return nc


def phase_c_body(P, T, a, yT):
    hT = a["hT"]
    NG = T // 512

    def const(name, shape, dt, src, **kw):
        t = P.sb(name, shape, dt)
        tk = P.tok(name)
        P.dma("sp", tk, writes=[tk], out=t[:], in_=src, **kw)
        return t, tk

    Wb = {}
    for nm in ("wr", "wk", "wv", "wg"):
        Wb[nm] = load_weight_bf16(P, a[nm], D, 256, "b_" + nm)
    for nm in ("w1", "a1"):
        Wb[nm] = load_weight_bf16(P, a[nm], D, 64, "b_" + nm, pad_to=128)
    W2 = {}
    for nm in ("w2", "a2"):
        f, f_t = const("f_" + nm, [64, 256], F32, a[nm])
        b = P.sb("b_" + nm, [128, 256], BF16)
        b_t = P.tok()
        P.op("pool", "memset", writes=[b_t], ap=b[:], constant=0.0)
        P.op("dve", "tensor_copy", reads=[f_t], writes=[b_t], out=b[0:64, :], in_=f[:])
        W2[nm] = (b, b_t)
    idb, idb_t = const("idb", [128, 128], BF16, a["ident"])
    mucol, mu_t = const("mucol", [128, 6, 8], F32, a["mu"].rearrange("m (kc p) -> p m kc", p=128),
                        allow_slow_non_contiguous=True)
    vbc, vbc_t = const("vbc", [128, 7, 256], F32, a["vecs"].partition_broadcast(128))
    cm, cm_t = const("cm_sb", [128, 9, 128], F32, a["cm"])
    mask4 = cm[:, 0:4, :]
    sl4 = cm[:, 4:8, :]
    tri = cm[:, 8, :]
    epsr = P.sb("epsr", [128, 1], F32)
    epsl = P.sb("epsl", [128, 1], F32)
    onesf = P.sb("onesf", [128, 1], F32)
    misc_t = P.tok("misc")
    P.op("dve", "memset", writes=[misc_t], ap=epsr[:], constant=0.0)
    P.op("dve", "memset", writes=[misc_t], ap=epsl[:], constant=LNX_EPS)
    P.op("dve", "memset", writes=[misc_t], ap=onesf[:], constant=1.0)

    hb = [P.sb(f"hb{i}", [128, 8, 513], BF16) for i in range(2)]
    hb_t = [P.tok() for _ in range(2)]
    xx = P.sb("xx", [128, 8, 512], F32)
    xx_t = P.tok("xx")
    xm = [P.sb(f"xm{i}", [128, 8, 512], BF16) for i in range(3)]
    xm_t = [P.tok() for _ in range(3)]
    NMS = ("R", "K", "V", "G", "WL", "AL")
    tb = {n: P.sb("tb_" + n, [128, 4, 256], F32) for n in NMS}
    tb_t = {n: P.tok("tb_" + n) for n in NMS}
    lt = {n: P.sb("lt_" + n, [128, 512], BF16) for n in ("w", "a")}
    lt_t = {n: P.tok() for n in ("w", "a")}
    yst = [P.sb(f"yst{i}", [128, 2, 512], BF16) for i in range(2)]
    yst_t = [P.tok() for _ in range(2)]

    def f32t(name, shape=(128, 256)):
        return P.sb(name, list(shape), F32), P.tok(name)
    LW, LW_t = f32t("LW")
    A_, A_t = f32t("A_")
    KKr, KKr_t = f32t("KKr")
    tmpa, tmpa_t = f32t("tmpa")
    tmpb, tmpb_t = f32t("tmpb")
    KP, KP_t = f32t("KP")
    Bv, Bv_t = f32t("Bv")
    ep, ep_t = f32t("ep")
    em, em_t = f32t("em")
    epv, epv_t = f32t("epv")
    bon, bon_t = f32t("bon")
    ycp, ycp_t = f32t("ycp")
    ysq, ysq_t = f32t("ysq")
    sgg, sgg_t = f32t("sgg")
    s4 = {n: f32t("s4_" + n, (128, 4)) for n in ("ssk", "bs", "s1", "s2", "m2", "var")}
    wc, wc_t = f32t("wc", (128, 2))
    TM = [P.sb(f"TM{i}", [128, 4, 256], BF16) for i in range(2)]
    TM_t = [P.tok() for _ in range(2)]
    Vb = [P.sb(f"Vb{i}", [128, 256], BF16) for i in range(2)]
    Vb_t = [P.tok() for _ in range(2)]
    FT = [P.sb(f"FT{i}", [128, 2, 4, 128], BF16) for i in range(2)]
    FT_t = [P.tok() for _ in range(2)]
    FTm = [P.sb(f"FTm{i}", [128, 4, 2, 128], BF16) for i in range(2)]
    FTm_t = [P.tok() for _ in range(2)]
    for i_ in range(2):
        P.op("pool", "memset", writes=[FTm_t[i_]], ap=FTm[i_][:], constant=0.0)
    MM = [P.sb(f"MM{i}", [128, 4, 4, 128], BF16) for i in range(2)]
    MM_t = [P.tok() for _ in range(2)]
    Pb = [P.sb(f"Pb{i}", [128, 4, 128], BF16) for i in range(2)]
    Pb_t = [P.tok() for _ in range(2)]
    Qb = [P.sb(f"Qb{i}", [128, 4, 128], BF16) for i in range(2)]
    Qb_t = [P.tok() for _ in range(2)]
    Tb = [P.sb(f"Tb{i}", [128, 4, 128], BF16) for i in range(2)]
    Tb_t = [P.tok() for _ in range(2)]
    TF = [P.sb(f"TF{i}", [128, 4, 128], BF16) for i in range(2)]
    TF_t = [P.tok() for _ in range(2)]
    XT = P.sb("XT", [128, 256], BF16)
    XT_t = P.tok()
    UT = P.sb("UT", [128, 256], BF16)
    UT_t = P.tok()
    STf = P.sb("STf", [128, 2, 64], F32)
    STw = P.sb("STw", [128, 2, 64], F32)
    STb = P.sb("STb", [128, 2, 64], BF16)
    STf_t, STw_t, STb_t = P.tok(), P.tok(), P.tok()
    yo = P.sb("yo", [128, 256], BF16)
    yo_t = P.tok()
    P.op("dve", "memset", writes=[STf_t], ap=STf[:], constant=0.0)
    P.op("dve", "memset", writes=[STb_t], ap=STb[:], constant=0.0)

    pj = P.ps("pj", [128, 512], F32)
    pj_t = P.tok()
    pcum = P.ps("pcum", [128, 512], F32)
    pcum_t = P.tok()
    ptr = P.ps("ptr", [128, 8, 128], BF16)
    ptr_t = P.tok()
    pA = P.ps("pA", [128, 4, 128], F32)
    pB = P.ps("pB", [128, 4, 128], F32)
    pC = P.ps("pC", [128, 4, 128], F32)
    pA_t, pB_t, pC_t = P.tok(), P.tok(), P.tok()
    pXU = P.ps("pXU", [128, 512], F32)
    pXU_t = P.tok()
    pYS = P.ps("pYS", [128, 512], F32)
    pYS_t = P.tok()

    def hview(ap4):
        return ap4.rearrange("p (h c) -> p h c", h=4)

    def bc4(ap):
        return ap.unsqueeze(2).broadcast_to([128, 4, 64])

    alt = [0]

    def ve():
        alt[0] += 1
        return "dve" if alt[0] % 2 else "pool"

    xm_ctr = [0]
    for gi in range(NG):
        gs = gi % 2
        t0 = gi * 512
        H = hb[gs]
        if gi == 0:
            P.op("dve", "memset", writes=[hb_t[gs]], ap=H[:, :, 0:1], constant=0.0)
            P.dma("sp", hb_t[gs], writes=[hb_t[gs]], out=H[:, :, 1:513],
                  in_=hT[:, :, 0:512].rearrange("kc p t -> p kc t"))
        else:
            P.dma("sp", hb_t[gs], writes=[hb_t[gs]], out=H[:, :, :],
                  in_=hT[:, :, t0 - 1:t0 + 512].rearrange("kc p t -> p kc t"))
        for half in range(2):
            ks = slice(half * 4, half * 4 + 4)
            P.op("dve" if half == 0 else "pool", "tensor_tensor", reads=[hb_t[gs]], writes=[xx_t],
                 out=xx[:, ks, :], in0=H[:, ks, 0:512], in1=H[:, ks, 1:513], op=ALU.subtract)
        for m, nm in enumerate(("R", "WL", "K", "V", "AL", "G")):
            xs = xm_ctr[0] % 3
            xm_ctr[0] += 1
            for kc in range(8):
                P.op("dve", "scalar_tensor_tensor", reads=[xx_t, hb_t[gs], mu_t], writes=[xm_t[xs]],
                     out=xm[xs][:, kc, :], in0=xx[:, kc, :], scalar=mucol[:, m, kc:kc + 1],
                     in1=H[:, kc, 1:513], op0=ALU.mult, op1=ALU.add)
            if nm in ("R", "K", "V", "G"):
                wbt, wbt_t = Wb["w" + nm.lower()]
                for tt in range(4):
                    for kc in range(8):
                        P.op("pe", "matmul", reads=[xm_t[xs], wbt_t], writes=[pj_t],
                             out=pj[:, 0:256], lhsT=xm[xs][:, kc, tt * 128:(tt + 1) * 128], rhs=wbt[:, kc, :],
                             start=(kc == 0), stop=(kc == 7))
                    P.op("act", "activation", reads=[pj_t], writes=[tb_t[nm]], out=tb[nm][:, tt, :],
                         in_=pj[:, 0:256], func=AF.Copy)
            else:
                ln_ = "w" if nm == "WL" else "a"
                w1b, w1b_t = Wb[ln_ + "1"]
                w2b, w2b_t = W2[ln_ + "2"]
                for kc in range(8):
                    P.op("pe", "matmul", reads=[xm_t[xs], w1b_t], writes=[pj_t],
                         out=pj[:, :], lhsT=w1b[:, kc, :], rhs=xm[xs][:, kc, :],
                         start=(kc == 0), stop=(kc == 7))
                P.op("act", "activation", reads=[pj_t], writes=[lt_t[ln_]], out=lt[ln_][:], in_=pj[:, :],
                     func=(AF.Tanh if ln_ == "w" else AF.Copy))
                vi = V_W0 if ln_ == "w" else V_A0
                for tt in range(4):
                    P.op("pe", "matmul", reads=[lt_t[ln_], w2b_t], writes=[pj_t],
                         out=pj[:, 0:256], lhsT=lt[ln_][:, tt * 128:(tt + 1) * 128], rhs=w2b[:],
                         start=True, stop=True)
                    P.op("dve", "tensor_tensor", reads=[pj_t, vbc_t], writes=[tb_t[nm]], out=tb[nm][:, tt, :],
                         in0=pj[:, 0:256], in1=vbc[:, vi, :], op=ALU.add)

        for tt in range(4):
            ci = gi * 4 + tt
            cs = ci % 2
            R, K, V, G, WL, AL = (tb[n][:, tt, :] for n in NMS)
            rd = lambda *n: [tb_t[x] for x in n]
            P.mark("c_s1")
            P.op("act", "activation", reads=rd("WL"), writes=[LW_t], out=LW[:], in_=WL, func=AF.Sigmoid)
            P.op("dve", "tensor_scalar", reads=[LW_t], writes=[LW_t], out=LW[:], in0=LW[:],
                 scalar1=-DECAY_C, scalar2=None, op0=ALU.mult)
            P.op("act", "activation", reads=rd("AL"), writes=[A_t], out=A_[:], in_=AL, func=AF.Sigmoid)
            P.op("pe", "matmul", reads=[LW_t, cm_t], writes=[pcum_t], out=pcum[:, 0:256], lhsT=tri, rhs=LW[:],
                 start=True, stop=True)
            for pr in range(2):
                P.op("pe", "matmul", reads=[LW_t, misc_t], writes=[pcum_t], out=pcum[:, 256 + pr:257 + pr],
                     lhsT=LW[:, pr * 128:(pr + 1) * 128], rhs=onesf[:], start=True, stop=True)
            P.mark("c_s2a")
            P.op("pool", "tensor_tensor", reads=rd("K") + [vbc_t], writes=[KKr_t], out=KKr[:], in0=K,
                 in1=vbc[:, V_KK, :], op=ALU.mult)
            P.op("pool", "tensor_tensor", reads=[KKr_t], writes=[tmpa_t], out=tmpa[:], in0=KKr[:], in1=KKr[:],
                 op=ALU.mult)
            ssk, ssk_t = s4["ssk"]
            P.op("dve", "tensor_reduce", reads=[tmpa_t], writes=[ssk_t], out=ssk[:], in_=hview(tmpa[:]),
                 axis=AX.X, op=ALU.add)
            P.op("dve", "tensor_scalar_max", reads=[ssk_t], writes=[ssk_t], out=ssk[:], in0=ssk[:], scalar1=1e-24)
            P.op("act", "activation", reads=[ssk_t], writes=[ssk_t], out=ssk[:], in_=ssk[:], func=AF.Sqrt)
            P.op("dve", "reciprocal", reads=[ssk_t], writes=[ssk_t], out=ssk[:], in_=ssk[:])
            P.op("dve", "tensor_tensor", reads=[KKr_t, ssk_t], writes=[KKr_t], out=hview(KKr[:]),
                 in0=hview(KKr[:]), in1=bc4(ssk[:]), op=ALU.mult)
            P.op("dve", "scalar_tensor_tensor", reads=[A_t, vbc_t], writes=[tmpb_t], out=tmpb[:], in0=A_[:],
                 scalar=-1.0, in1=vbc[:, V_KA, :], op0=ALU.add, op1=ALU.mult)
            P.op("dve", "scalar_tensor_tensor", reads=[tmpb_t] + rd("K"), writes=[KP_t], out=KP[:], in0=tmpb[:],
                 scalar=1.0, in1=K, op0=ALU.add, op1=ALU.mult)
            P.op("pool", "tensor_tensor", reads=[KKr_t, A_t], writes=[Bv_t], out=Bv[:], in0=KKr[:], in1=A_[:],
                 op=ALU.mult)
            P.op("pool", "tensor_tensor", reads=rd("R") + [KP_t], writes=[tmpa_t], out=tmpa[:], in0=R, in1=KP[:],
                 op=ALU.mult)
            P.op("pool", "tensor_tensor", reads=[tmpa_t, vbc_t], writes=[tmpa_t], out=tmpa[:], in0=tmpa[:],
                 in1=vbc[:, V_RK, :], op=ALU.mult)
            bs, bs_t = s4["bs"]
            P.op("dve", "tensor_reduce", reads=[tmpa_t], writes=[bs_t], out=bs[:], in_=hview(tmpa[:]),
                 axis=AX.X, op=ALU.add)
            P.op("dve", "tensor_tensor", reads=rd("V") + [bs_t], writes=[bon_t], out=hview(bon[:]),
                 in0=hview(V), in1=bc4(bs[:]), op=ALU.mult)
            P.mark("c_s2b")
            P.op("act", "activation", reads=[pcum_t], writes=[ep_t], out=ep[:], in_=pcum[:, 0:256], func=AF.Exp)
            P.op("act", "activation", reads=[pcum_t], writes=[em_t], out=em[:], in_=pcum[:, 0:256], func=AF.Exp,
                 scale=-1.0)
            P.op("act", "activation", reads=[LW_t], writes=[epv_t], out=epv[:], in_=LW[:], func=AF.Exp, scale=-1.0)
            P.op("pool", "tensor_tensor", reads=[epv_t, ep_t], writes=[epv_t], out=epv[:], in0=epv[:], in1=ep[:],
                 op=ALU.mult)
            P.op("act", "activation", reads=[pcum_t], writes=[wc_t], out=wc[:], in_=pcum[:, 256:258], func=AF.Exp)
            P.op("dve", "scalar_tensor_tensor", reads=[KKr_t, epv_t], writes=[TM_t[cs]], out=TM[cs][:, 0, :],
                 in0=KKr[:], scalar=-1.0, in1=epv[:], op0=ALU.mult, op1=ALU.mult)
            P.op("dve", "tensor_tensor", reads=rd("R") + [ep_t], writes=[TM_t[cs]], out=TM[cs][:, 1, :], in0=R,
                 in1=ep[:], op=ALU.mult)
            P.op("pool", "tensor_tensor", reads=[Bv_t, em_t], writes=[TM_t[cs]], out=TM[cs][:, 2, :], in0=Bv[:],
                 in1=em[:], op=ALU.mult)
            P.op("dve", "tensor_tensor", reads=[KP_t, em_t], writes=[TM_t[cs]], out=TM[cs][:, 3, :], in0=KP[:],
                 in1=em[:], op=ALU.mult)
            P.op("pool", "tensor_copy", reads=rd("V"), writes=[Vb_t[cs]], out=Vb[cs][:], in_=V)
            for pr in range(2):
                for q in range(4):
                    P.op("pe", "transpose", reads=[TM_t[cs], idb_t], writes=[ptr_t],
                         out=ptr[:, pr * 4 + q, :], in_=TM[cs][:, q, pr * 128:(pr + 1) * 128], identity=idb[:])
            P.op("act", "activation", reads=[ptr_t], writes=[FT_t[cs]],
                 out=FT[cs][:].rearrange("p a q t -> p (a q t)"), in_=ptr[:].rearrange("p a t -> p (a t)"),
                 func=AF.Copy)
            P.mark("c_s2c")
            for wh in range(2):
                hp = slice(wh * 64, wh * 64 + 64)
                for pr in range(2):
                    P.op("act", "activation", reads=[ptr_t], writes=[FTm_t[cs]],
                         out=FTm[cs][hp, pr * 2 + wh, :, :], in_=ptr[hp, pr * 4:pr * 4 + 2, :], func=AF.Copy)
            P.mark("c_s2")
            F = FT[cs]
            Fm = FTm[cs]
            for h in range(4):
                hp = slice((h % 2) * 64, (h % 2) * 64 + 64)
                pr = h // 2
                bank, bank_t = (pA, pA_t) if h % 2 == 0 else (pB, pB_t)
                rhs2 = Fm[:, h, :, :].rearrange("p q t -> p (q t)")
                bf = bank[:].rearrange("p a t -> p (a t)")
                P.op("pe", "matmul", reads=[FT_t[cs], FTm_t[cs]], writes=[bank_t], out=bf[:, 0:256],
                     lhsT=F[:, pr, 2, :], rhs=rhs2, start=True, stop=True)
                P.op("pe", "matmul", reads=[FT_t[cs], FTm_t[cs]], writes=[bank_t], out=bf[:, 256:512],
                     lhsT=F[:, pr, 3, :], rhs=rhs2, start=True, stop=True)
                P.op("dve", "tensor_tensor", reads=[bank_t, cm_t], writes=[MM_t[cs]], out=MM[cs][:, h, :, :],
                     in0=bank[:], in1=mask4, op=ALU.mult)
            for h in range(4):
                hp = slice((h % 2) * 64, (h % 2) * 64 + 64)
                pr = h // 2
                P.op("pe", "matmul", reads=[FT_t[cs], FTm_t[cs]], writes=[pC_t], out=pC[:, h, :],
                     lhsT=Fm[:, h, 0, :], rhs=F[:, pr, 2, :], start=True, stop=True)
            P.op("dve", "tensor_tensor", reads=[pC_t, cm_t], writes=[Qb_t[0]], out=Qb[0][:], in0=pC[:], in1=sl4,
                 op=ALU.mult)
            M = MM[cs]
            P.op("pool", "tensor_tensor", reads=[MM_t[cs], idb_t], writes=[Tb_t[0]], out=Tb[0][:],
                 in0=M[:, :, 0, :], in1=idb[:].unsqueeze(1).broadcast_to([128, 4, 128]), op=ALU.add)
            P.mark("c_s3")
            NL = 6
            for lv in range(1, NL + 1):
                prv, cur = (lv - 1) % 2, lv % 2
                last = lv == NL
                Pprev = (lambda h: M[:, h, 0, :]) if lv == 1 else (lambda h, prv=prv: Pb[prv][:, h, :])
                Pprev_t = MM_t[cs] if lv == 1 else Pb_t[prv]
                if not last:
                    for h in range(4):
                        P.op("pe", "matmul", reads=[Pprev_t, Qb_t[prv]], writes=[pA_t], out=pA[:, h, :],
                             lhsT=Qb[prv][:, h, :], rhs=Pprev(h), start=True, stop=True)
                for h in range(4):
                    P.op("pe", "matmul", reads=[Pprev_t, Qb_t[prv]], writes=[pB_t], out=pB[:, h, :],
                         lhsT=Pprev(h), rhs=Qb[prv][:, h, :], start=True, stop=True)
                if not last:
                    P.op("act", "activation", reads=[pA_t], writes=[Pb_t[cur]], out=Pb[cur][:], in_=pA[:],
                         func=AF.Copy)
                P.op("dve", "tensor_copy", reads=[pB_t], writes=[Qb_t[cur]], out=Qb[cur][:], in_=pB[:])
                for h in range(4):
                    P.op("pe", "matmul", reads=[Qb_t[cur], Tb_t[prv]], writes=[pC_t], out=pC[:, h, :],
                         lhsT=Qb[cur][:, h, :], rhs=Tb[prv][:, h, :], start=True, stop=True)
                dst, dst_t = (TF[cs], TF_t[cs]) if last else (Tb[cur], Tb_t[cur])
                P.op("dve", "tensor_tensor", reads=[pC_t, Tb_t[prv]], writes=[dst_t], out=dst[:], in0=pC[:],
                     in1=Tb[prv][:], op=ALU.add)
            P.mark("c_dbl")
            X = pXU[:, 0:256]
            U = pXU[:, 256:512]
            Y = pYS[:, 0:256]
            S4 = pYS[:, 256:512].rearrange("p (a w v) -> p a w v", a=2, w=2)
            for h in range(4):
                hp = slice((h % 2) * 64, (h % 2) * 64 + 64)
                pr = h // 2
                hc = slice(h * 64, (h + 1) * 64)
                P.op("pe", "matmul", reads=[FTm_t[cs], STb_t], writes=[pXU_t], out=X[:, hc], lhsT=Fm[:, h, 0, :],
                     rhs=STb[:, pr, :], start=True, stop=False)
                P.op("pe", "matmul", reads=[MM_t[cs], Vb_t[cs]], writes=[pXU_t], out=X[:, hc], lhsT=M[:, h, 2, :],
                     rhs=Vb[cs][:, hc], start=False, stop=True)
            P.op("act", "activation", reads=[pXU_t], writes=[XT_t], out=XT[:], in_=X, func=AF.Copy)
            for h in range(4):
                hc = slice(h * 64, (h + 1) * 64)
                P.op("pe", "matmul", reads=[TF_t[cs], XT_t], writes=[pXU_t], out=U[:, hc], lhsT=TF[cs][:, h, :],
                     rhs=XT[:, hc], start=True, stop=True)
            P.op("act", "activation", reads=[pXU_t], writes=[UT_t], out=UT[:], in_=U, func=AF.Copy)
            P.op("dve", "tensor_tensor", reads=[STf_t, wc_t], writes=[STw_t], out=STw[:], in0=STf[:],
                 in1=wc[:].unsqueeze(2).broadcast_to([128, 2, 64]), op=ALU.mult)
            for pr in range(2):
                for wh in range(2):
                    h = pr * 2 + wh
                    hc = slice(h * 64, (h + 1) * 64)
                    pc = slice(pr * 128, (pr + 1) * 128)
                    P.op("pe", "matmul", reads=[TM_t[cs], UT_t], writes=[pYS_t], out=S4[:, pr, wh, :],
                         lhsT=TM[cs][:, 2, pc], rhs=UT[:, hc], start=True, stop=False)
                    P.op("pe", "matmul", reads=[TM_t[cs], Vb_t[cs]], writes=[pYS_t], out=S4[:, pr, wh, :],
                         lhsT=TM[cs][:, 3, pc], rhs=Vb[cs][:, hc], start=False, stop=True)
            for h in range(4):
                hp = slice((h % 2) * 64, (h % 2) * 64 + 64)
                pr = h // 2
                hc = slice(h * 64, (h + 1) * 64)
                P.op("pe", "matmul", reads=[FTm_t[cs], STb_t], writes=[pYS_t], out=Y[:, hc], lhsT=Fm[:, h, 1, :],
                     rhs=STb[:, pr, :], start=True, stop=False)
                P.op("pe", "matmul", reads=[MM_t[cs], UT_t], writes=[pYS_t], out=Y[:, hc], lhsT=M[:, h, 1, :],
                     rhs=UT[:, hc], start=False, stop=False)
                P.op("pe", "matmul", reads=[MM_t[cs], Vb_t[cs]], writes=[pYS_t], out=Y[:, hc], lhsT=M[:, h, 3, :],
                     rhs=Vb[cs][:, hc], start=False, stop=True)
            for pr in range(2):
                for wh in range(2):
                    hp = slice(wh * 64, wh * 64 + 64)
                    P.op("dve", "scalar_tensor_tensor", reads=[pYS_t, wc_t, STw_t], writes=[STf_t],
                         out=STf[hp, pr, :], in0=S4[hp, pr, wh, :], scalar=wc[hp, pr:pr + 1], in1=STw[hp, pr, :],
                         op0=ALU.mult, op1=ALU.add)
            P.op("dve", "tensor_copy", reads=[STf_t], writes=[STb_t], out=STb[:], in_=STf[:])
            P.mark("c_seq")
            P.op("dve", "tensor_copy", reads=[pYS_t], writes=[ycp_t], out=ycp[:], in_=Y)
            s1, s1_t = s4["s1"]
            s2, s2_t = s4["s2"]
            m2, m2_t = s4["m2"]
            var, var_t = s4["var"]
            P.op("dve", "tensor_reduce", reads=[ycp_t], writes=[s1_t], out=s1[:], in_=hview(ycp[:]), axis=AX.X,
                 op=ALU.add)
            P.op("pool", "tensor_tensor", reads=[ycp_t], writes=[ysq_t], out=ysq[:], in0=ycp[:], in1=ycp[:],
                 op=ALU.mult)
            P.op("dve", "tensor_reduce", reads=[ysq_t], writes=[s2_t], out=s2[:], in_=hview(ysq[:]), axis=AX.X,
                 op=ALU.add)
            P.op("dve", "tensor_scalar", reads=[s1_t], writes=[s1_t], out=s1[:], in0=s1[:], scalar1=1.0 / 64,
                 scalar2=None, op0=ALU.mult)
            P.op("dve", "tensor_tensor", reads=[s1_t], writes=[m2_t], out=m2[:], in0=s1[:], in1=s1[:], op=ALU.mult)
            P.op("dve", "scalar_tensor_tensor", reads=[s2_t, m2_t], writes=[var_t], out=var[:], in0=s2[:],
                 scalar=1.0 / 64, in1=m2[:], op0=ALU.mult, op1=ALU.subtract)
            P.op("act", "activation", reads=[var_t, misc_t], writes=[var_t], out=var[:], in_=var[:], func=AF.Sqrt,
                 bias=epsl[:], scale=1.0)
            P.op("dve", "reciprocal", reads=[var_t], writes=[var_t], out=var[:], in_=var[:])
            P.op("dve", "tensor_tensor", reads=[ycp_t, s1_t], writes=[ycp_t], out=hview(ycp[:]), in0=hview(ycp[:]),
                 in1=bc4(s1[:]), op=ALU.subtract)
            P.op("dve", "tensor_tensor", reads=[ycp_t, var_t], writes=[ycp_t], out=hview(ycp[:]), in0=hview(ycp[:]),
                 in1=bc4(var[:]), op=ALU.mult)
            P.op("pool", "tensor_tensor", reads=[ycp_t, vbc_t], writes=[ycp_t], out=ycp[:], in0=ycp[:],
                 in1=vbc[:, V_LNW, :], op=ALU.mult)
            P.op("pool", "tensor_tensor", reads=[ycp_t, vbc_t], writes=[ycp_t], out=ycp[:], in0=ycp[:],
                 in1=vbc[:, V_LNB, :], op=ALU.add)
            P.op("pool", "tensor_tensor", reads=[ycp_t, bon_t], writes=[ycp_t], out=ycp[:], in0=ycp[:], in1=bon[:],
                 op=ALU.add)
            P.op("act", "activation", reads=rd("G"), writes=[sgg_t], out=sgg[:], in_=G, func=AF.Silu)
            P.op("pool", "tensor_tensor", reads=[ycp_t, sgg_t], writes=[yo_t], out=yo[:], in0=ycp[:], in1=sgg[:],
                 op=ALU.mult)
            for pr in range(2):
                P.op("pe", "transpose", reads=[yo_t, idb_t], writes=[ptr_t], out=ptr[:, pr, :],
                     in_=yo[:, pr * 128:(pr + 1) * 128], identity=idb[:])
            P.op("act", "activation", reads=[ptr_t], writes=[yst_t[gs]], out=yst[gs][:, :, tt * 128:(tt + 1) * 128],
                 in_=ptr[:, 0:2, :], func=AF.Copy)
        P.dma("pool", yst_t[gs], reads=[yst_t[gs]], out=yT[:, :, t0:t0 + 512].rearrange("c p t -> p c t"),
              in_=yst[gs][:])


SEQ = 16384
NCORES = 8


def _win_cols(g):
    c = []
    for base in (0, 512, 1024, 1536, 2048, 2560, 3072):
        c += list(range(base + g * 128, base + (g + 1) * 128))
    return np.array(c)


def _etab_for(g):
    slopes = 2.0 ** (-8.0 * np.arange(1, 9) / 8)
    E = np.zeros((128, 3, 512), np.float64)
    kl = np.arange(128)[:, None]
    qi = np.arange(128)[None, :]
    for p, d in enumerate(PATTERNS):
        for hh in range(2):
            s = slopes[2 * g + hh]
            E[:, p, hh * 256:hh * 256 + 128] = np.where(kl >= qi, np.exp(-s * d * (qi - kl + 128)), 0.0)
            E[:, p, hh * 256 + 128:hh * 256 + 256] = np.where(qi >= kl, np.exp(-s * d * (qi - kl)), 0.0)
    return E.astype(np.float32)


def _c_masks():
    i = np.arange(128)[:, None]
    t = np.arange(128)[None, :]
    SU = (i < t).astype(np.float32)
    IU = (i <= t).astype(np.float32)
    SL = (t < i).astype(np.float32)
    return np.ascontiguousarray(np.stack([SU, IU, SU, IU, SL, SL, SL, SL, IU], 1))


_PROGS = {}


def _prog(key, fn):
    if key not in _PROGS:
        _PROGS[key] = fn()
    return _PROGS[key]


def kernel(x, ln_even, w_in_even, gm_norm, gm_ws, gm_b, w_out_even,
           ln_odd, rw_mu, rw_wr, rw_wk, rw_wv, rw_wg, rw_w0, rw_w1, rw_w2,
           rw_a0, rw_a1, rw_a2, rw_kk, rw_ka, rw_rk, rw_lnw, rw_lnb, rw_wo,
           final_norm):
    f = lambda a: np.ascontiguousarray(np.asarray(a, dtype=np.float32))
    x = f(x)
    B, T, _ = x.shape
    TS = T // 4
    cores = list(range(NCORES))
    ident = np.eye(128).astype(NPBF16)
    triu = np.triu(np.ones((128, 128), np.float32))
    w_in = f(w_in_even)[0]

    nca = _prog(("a", T), lambda: build_phase_a(T))
    im = []
    for c in cores:
        b, g = divmod(c, 4)
        im.append(dict(x=x[b], ln=f(ln_even)[0], w_in_s=np.ascontiguousarray(w_in[:, _win_cols(g)]),
                       gmn=f(gm_norm)[0, g], gwsT=np.ascontiguousarray(f(gm_ws)[0, g].T), gmb=f(gm_b)[0, g],
                       triu=triu, etab=_etab_for(g), ident=ident))
    res = run_bass_kernel_spmd(nca, im, core_ids=cores).results
    y0 = np.zeros((B, 8, 128, T), NPBF16)
    for c in cores:
        b, g = divmod(c, 4)
        y0[b, g] = res[c]["yT"][0]
        y0[b, 4 + g] = res[c]["yT"][1]

    ncb = _prog(("b", TS, "mid"), lambda: build_phase_b(TS, "mid"))
    im = []
    for c in cores:
        b, q = divmod(c, 4)
        ts = slice(q * TS, (q + 1) * TS)
        im.append(dict(yT=np.ascontiguousarray(y0[b][:, :, ts]), xres=np.ascontiguousarray(x[b, ts]),
                       w=f(w_out_even)[0], g=f(ln_odd)[0], ident=ident))
    res = run_bass_kernel_spmd(ncb, im, core_ids=cores).results
    x1 = [res[c]["xo"] for c in cores]
    h1 = np.zeros((B, 8, 128, T), NPBF16)
    for c in cores:
        b, q = divmod(c, 4)
        h1[b][:, :, q * TS:(q + 1) * TS] = res[c]["hT"]

    ncc = _prog(("c", T), lambda: build_phase_c(T))
    cm = _c_masks()
    im = []
    for c in cores:
        b, g = divmod(c, 4)
        cs = slice(g * 256, (g + 1) * 256)
        sl = lambda a: np.ascontiguousarray(f(a)[0][:, cs])
        vecs = np.stack([f(rw_w0)[0, cs], f(rw_a0)[0, cs], f(rw_kk)[0, cs], f(rw_ka)[0, cs],
                         f(rw_rk)[0].reshape(-1)[cs], f(rw_lnw)[0, cs], f(rw_lnb)[0, cs]]).astype(np.float32)
        im.append(dict(hT=h1[b], mu=f(rw_mu)[0], wr=sl(rw_wr), wk=sl(rw_wk), wv=sl(rw_wv), wg=sl(rw_wg),
                       w1=f(rw_w1)[0], a1=f(rw_a1)[0], w2=sl(rw_w2), a2=sl(rw_a2),
                       vecs=np.ascontiguousarray(vecs), cm=cm, ident=ident))
    res = run_bass_kernel_spmd(ncc, im, core_ids=cores).results
    y1 = np.zeros((B, 8, 128, T), NPBF16)
    for c in cores:
        b, g = divmod(c, 4)
        y1[b, 2 * g] = res[c]["yT"][0]
        y1[b, 2 * g + 1] = res[c]["yT"][1]

    ncd = _prog(("b", TS, "fin"), lambda: build_phase_b(TS, "fin"))
    im = []
    for c in cores:
        b, q = divmod(c, 4)
        ts = slice(q * TS, (q + 1) * TS)
        im.append(dict(yT=np.ascontiguousarray(y1[b][:, :, ts]), xres=x1[c], w=f(rw_wo)[0], g=f(final_norm),
                       ident=ident))
    res = run_bass_kernel_spmd(ncd, im, core_ids=cores).results
    out = np.zeros((B, T, D), np.float32)
    for c in cores:
        b, q = divmod(c, 4)
        out[b, q * TS:(q + 1) * TS] = res[c]["out"]
    return out
```
